# Optimizing a Trainium2 kernel written in Bass

```python
import math
import jax, jax.numpy as jnp
from jax import lax
import numpy as np

D_MODEL = 2048
BATCH = 2
SEQ = 8192
DEPTH = 2

N_MIXERS = 2
ATT_GROUPS = ((128, 1), (512, 4), (2048, 16))
N_GROUPS = len(ATT_GROUPS)
HEAD_DIM = 128
HEADS_PER_GROUP = D_MODEL // HEAD_DIM
ATT_WIDTH = HEADS_PER_GROUP * HEAD_DIM
A_IN_COLS = 3 * N_GROUPS * ATT_WIDTH + ATT_WIDTH
BLOCK = 128
N_BUCKETS = 32
MAX_DISTANCE = 2048
CONV_WIDTH = 31
CONV_CH = D_MODEL
B_IN_COLS = 3 * CONV_CH
EPS = 1e-6
N_A = (DEPTH + 1) // 2
N_B = DEPTH // 2

kernel_name = "hybrid_dilated_attn_conformer_conv"


def _rmsnorm(x, g):
    xf = x.astype(jnp.float32)
    y = xf * lax.rsqrt(jnp.mean(xf * xf, axis=-1, keepdims=True) + EPS) * g.astype(jnp.float32)
    return y.astype(x.dtype)


def _head_rms(t, g):
    tf = t.astype(jnp.float32)
    return tf * lax.rsqrt(jnp.mean(tf * tf, axis=-1, keepdims=True) + EPS) * g.astype(jnp.float32)


def _t5_bucket(dist):
    max_exact = N_BUCKETS // 2
    is_small = dist < max_exact
    ratio = jnp.log(jnp.maximum(dist, 1).astype(jnp.float32) / max_exact) / math.log(MAX_DISTANCE / max_exact)
    large = max_exact + (ratio * (N_BUCKETS - max_exact)).astype(jnp.int32)
    large = jnp.minimum(large, N_BUCKETS - 1)
    return jnp.where(is_small, dist, large)


def _dilated_window_attention(q, k, v, bias_table, window, dilation):
    B, S, H, Dh = q.shape
    steps = window // dilation
    L = S // dilation
    Lp = -(-L // BLOCK) * BLOCK
    nb = Lp // BLOCK

    def gather(t):
        t = t.reshape(B, L, dilation, H, Dh).transpose(0, 2, 1, 3, 4).reshape(B * dilation, L, H, Dh)
        t = jnp.pad(t, ((0, 0), (0, Lp - L), (0, 0), (0, 0)))
        return t.reshape(B * dilation, nb, BLOCK, H, Dh)

    def with_prev(t):
        prev = jnp.pad(t[:, :-1], ((0, 0), (1, 0), (0, 0), (0, 0), (0, 0)))
        return jnp.concatenate([prev, t], axis=2)

    qb = gather(q)
    kk = with_prev(gather(k))
    vv = with_prev(gather(v))

    s = jnp.einsum('bnqhd,bnkhd->bnhqk', qb, kk) * (Dh ** -0.5)
    qi = jnp.arange(BLOCK)[:, None] + BLOCK
    kj = jnp.arange(2 * BLOCK)[None, :]
    step = qi - kj
    in_window = (step >= 0) & (step <= steps)
    bucket = _t5_bucket(jnp.maximum(step, 0) * dilation)
    bias = bias_table.astype(jnp.float32)[bucket].transpose(2, 0, 1)
    blk_ok = (jnp.arange(nb)[:, None, None] > 0) | (kj[None] >= BLOCK)
    mask = in_window[None] & blk_ok
    s = jnp.where(mask[None, :, None], s + bias[None, None], -jnp.inf)
    m = jnp.max(s, axis=-1, keepdims=True)
    p = jnp.exp(s - m)
    l = jnp.sum(p, axis=-1, keepdims=True)
    o = jnp.einsum('bnhqk,bnkhd->bnqhd', p / l, vv)
    lse = (m + jnp.log(l))[..., 0].transpose(0, 1, 3, 2)

    o = o.reshape(B * dilation, Lp, H, Dh)[:, :L]
    o = o.reshape(B, dilation, L, H, Dh).transpose(0, 2, 1, 3, 4).reshape(B, S, H, Dh)
    lse = lse.reshape(B * dilation, Lp, H)[:, :L]
    lse = lse.reshape(B, dilation, L, H).transpose(0, 2, 1, 3).reshape(B, S, H)
    return o, lse


def _mixer_a(h, w_in, q_gain, k_gain, w_out, rel_bias):
    B, S, _ = h.shape
    proj = h @ w_in
    qkv = proj[..., :3 * N_GROUPS * ATT_WIDTH].reshape(B, S, N_GROUPS, 3, HEADS_PER_GROUP, HEAD_DIM)
    z = proj[..., 3 * N_GROUPS * ATT_WIDTH:]
    outs, lses = [], []
    for g, (window, dilation) in enumerate(ATT_GROUPS):
        q = _head_rms(qkv[:, :, g, 0], q_gain[g])
        k = _head_rms(qkv[:, :, g, 1], k_gain[g])
        v = qkv[:, :, g, 2].astype(jnp.float32)
        table = rel_bias[:, g * HEADS_PER_GROUP:(g + 1) * HEADS_PER_GROUP]
        o, lse = _dilated_window_attention(q, k, v, table, window, dilation)
        outs.append(o)
        lses.append(lse)
    w = jax.nn.softmax(jnp.stack(lses, axis=0), axis=0)
    o = jnp.sum(w[..., None] * jnp.stack(outs, axis=0), axis=0)
    y = o.reshape(B, S, ATT_WIDTH) * jax.nn.silu(z.astype(jnp.float32))
    return y.astype(h.dtype) @ w_out


def _mixer_b(h, w_in, b_in, conv_w, conv_b, ln_g, ln_b, w_out, b_out):
    proj = h @ w_in + b_in
    a, ga, z = jnp.split(proj, 3, axis=-1)
    u = a * jax.nn.sigmoid(ga)
    u = lax.conv_general_dilated(
        u, conv_w[:, None, :].astype(u.dtype), window_strides=(1,),
        padding=((CONV_WIDTH - 1, 0),), dimension_numbers=('NWC', 'WIO', 'NWC'),
        feature_group_count=CONV_CH) + conv_b
    uf = u.astype(jnp.float32)
    mu = jnp.mean(uf, axis=-1, keepdims=True)
    var = jnp.mean(jnp.square(uf - mu), axis=-1, keepdims=True)
    uf = (uf - mu) * lax.rsqrt(var + EPS) * ln_g.astype(jnp.float32) + ln_b.astype(jnp.float32)
    y = jax.nn.silu(uf) * jax.nn.silu(z.astype(jnp.float32))
    return y.astype(h.dtype) @ w_out + b_out


def setup_inputs(seed: int = 0) -> dict:
    key = jax.random.key(seed)
    ks = jax.random.split(key, 16)
    f32 = jnp.float32
    nrm = lambda k, shape, scale: jax.random.normal(k, shape, f32) * scale
    return {
        "x": nrm(ks[0], (BATCH, SEQ, D_MODEL), 1.0),
        "norm_g": 1.0 + nrm(ks[1], (DEPTH, D_MODEL), 0.02),
        "rel_bias": nrm(ks[2], (N_BUCKETS, N_GROUPS * HEADS_PER_GROUP), 0.3),
        "a_w_in": nrm(ks[3], (N_A, D_MODEL, A_IN_COLS), D_MODEL ** -0.5),
        "a_q_gain": 1.0 + nrm(ks[4], (N_A, N_GROUPS, HEAD_DIM), 0.02),
        "a_k_gain": 1.0 + nrm(ks[5], (N_A, N_GROUPS, HEAD_DIM), 0.02),
        "a_w_out": nrm(ks[6], (N_A, ATT_WIDTH, D_MODEL), ATT_WIDTH ** -0.5),
        "b_w_in": nrm(ks[7], (N_B, D_MODEL, B_IN_COLS), D_MODEL ** -0.5),
        "b_b_in": nrm(ks[8], (N_B, B_IN_COLS), 0.02),
        "b_conv_w": nrm(ks[9], (N_B, CONV_WIDTH, CONV_CH), CONV_WIDTH ** -0.5),
        "b_conv_b": nrm(ks[10], (N_B, CONV_CH), 0.02),
        "b_ln_g": 1.0 + nrm(ks[11], (N_B, CONV_CH), 0.02),
        "b_ln_b": nrm(ks[12], (N_B, CONV_CH), 0.02),
        "b_w_out": nrm(ks[13], (N_B, CONV_CH, D_MODEL), CONV_CH ** -0.5),
        "b_b_out": nrm(ks[14], (N_B, D_MODEL), 0.02),
    }


def reference(x, norm_g, rel_bias, a_w_in, a_q_gain, a_k_gain, a_w_out,
              b_w_in, b_b_in, b_conv_w, b_conv_b, b_ln_g, b_ln_b, b_w_out, b_b_out):
    for i in range(DEPTH):
        h = _rmsnorm(x, norm_g[i])
        j = i // N_MIXERS
        if i % N_MIXERS == 0:
            x = x + _mixer_a(h, a_w_in[j], a_q_gain[j], a_k_gain[j], a_w_out[j], rel_bias)
        else:
            x = x + _mixer_b(h, b_w_in[j], b_b_in[j], b_conv_w[j], b_conv_b[j],
                             b_ln_g[j], b_ln_b[j], b_w_out[j], b_b_out[j])
    return x
```

```python
import math
from contextlib import ExitStack

import numpy as np
import concourse.bass as bass
import concourse.mybir as mybir
from concourse.bass_utils import run_bass_kernel_spmd

F32 = mybir.dt.float32
BF16 = mybir.dt.bfloat16
AF = mybir.ActivationFunctionType
ALU = mybir.AluOpType

D = 2048
NTOK = 2048
NH = 2048
KC = 16
EPS = 1e-6
GROUPS = ((128, 1), (512, 4), (2048, 16))
RG = (1, 4, 16)
NB = (16, 4, 1)
NEG = -30000.0
CONVW = 31


class Slot:
    __slots__ = ("name", "last_w", "readers", "dsem", "dcnt", "dlast")

    def __init__(self, name):
        self.name = name
        self.last_w = None
        self.readers = []
        self.dsem = None
        self.dcnt = 0
        self.dlast = None


class TR:
    ENG = ("pe", "act", "dve", "pool", "sp")

    def __init__(self, nc, es):
        self.nc = nc
        self.es = es
        self.q = {e: [] for e in self.ENG}
        self.esem = {e: es.enter_context(nc.semaphore("S_" + e)) for e in ("pe", "act", "dve", "pool")}
        self.ecnt = {e: 0 for e in self.ENG}
        self.known = {e: {} for e in self.ENG}
        self.dslots = []
        self.nsem = 0
        self.trace = {e: [] for e in self.ENG}
        self.phases = []

    def slot(self, name):
        return Slot(name)

    def _waits(self, eng, evs):
        need = {}
        for ev in evs:
            if ev is None:
                continue
            sem, v, src = ev
            if src == "pe" and eng == "pe":
                continue
            k = id(sem)
            if self.known[eng].get(k, 0) >= v:
                continue
            if k not in need or need[k][1] < v:
                need[k] = (sem, v)
        out = []
        for k, (sem, v) in need.items():
            self.known[eng][k] = v
            out.append((sem, v))
        return out

    def _deps(self, reads, writes):
        evs = []
        for s in reads:
            evs.append(s.last_w)
        for s in writes:
            evs.append(s.last_w)
            evs.extend(s.readers)
        return evs

    def _commit(self, ev, reads, writes):
        for s in reads:
            s.readers.append(ev)
        for s in writes:
            s.last_w = ev
            s.readers = []

    def op(self, eng, f, reads=(), writes=()):
        waits = self._waits(eng, self._deps(reads, writes))
        self.ecnt[eng] += 1
        sem = self.esem[eng]
        ev = (sem, self.ecnt[eng], eng)

        def thunk(E, waits=waits, f=f, sem=sem):
            for s, v in waits:
                E.wait_ge(s, v)
            f(E).then_inc(sem, 1)

        self.q[eng].append(thunk)
        self.trace[eng].append((waits, (sem, 1)))
        self._commit(ev, reads, writes)
        return ev

    def mm(self, fs, reads=(), writes=()):
        waits = self._waits("pe", self._deps(reads, writes))
        self.ecnt["pe"] += 1
        sem = self.esem["pe"]
        ev = (sem, self.ecnt["pe"], "pe")

        def thunk(E, waits=waits, fs=fs, sem=sem):
            for s, v in waits:
                E.wait_ge(s, v)
            ins = None
            for f in fs:
                ins = f(E)
            ins.then_inc(sem, 1)

        self.q["pe"].append(thunk)
        self.trace["pe"].append((waits, (sem, 1)))
        self._commit(ev, reads, writes)
        return ev

    def dma(self, eng, out, in_, anchor, reads=(), writes=()):
        if anchor.dsem is None:
            anchor.dsem = self.es.enter_context(self.nc.semaphore("D%d_%s" % (self.nsem, anchor.name)))
            self.nsem += 1
            self.dslots.append(anchor)
        evs = self._deps(reads, writes)
        evs.append(anchor.dlast)
        waits = self._waits(eng, evs)
        anchor.dcnt += 16
        sem = anchor.dsem
        ev = (sem, anchor.dcnt, None)
        anchor.dlast = ev

        def thunk(E, waits=waits, out=out, in_=in_, sem=sem):
            for s, v in waits:
                E.wait_ge(s, v)
            E.dma_start(out=out, in_=in_).then_inc(sem, 16)

        self.q[eng].append(thunk)
        self.trace[eng].append((waits, (sem, 16)))
        self._commit(ev, reads, writes)
        return ev

    def run(self):
        evs = [s.dlast for s in self.dslots]
        waits = self._waits("sp", evs)

        def fin(E, waits=waits):
            for s, v in waits:
                E.wait_ge(s, v)

        self.q["sp"].append(fin)
        self.trace["sp"].append((waits, None))
        self.phases.append(self.trace)
        self.trace = {e: [] for e in self.ENG}
        q = self.q
        self.q = {e: [] for e in self.ENG}
        with self.nc.Block() as block:
            @block.tensor
            def _(E):
                for t in q["pe"]:
                    t(E)

            @block.scalar
            def _(E):
                for t in q["act"]:
                    t(E)

            @block.vector
            def _(E):
                for t in q["dve"]:
                    t(E)

            @block.gpsimd
            def _(E):
                for t in q["pool"]:
                    t(E)

            @block.sync
            def _(E):
                for t in q["sp"]:
                    t(E)


class Bufs:
    def __init__(self, tr, es, name, shape, dtype, n, psum=False):
        self.t = []
        self.s = []
        for i in range(n):
            nm = "%s%d" % (name, i)
            if psum:
                assert dtype == F32 and len(shape) == 2 and shape[1] <= 512
                full = es.enter_context(tr.nc.psum_tensor(nm, [128, 512], F32))
                t = full[:, 0:shape[1]]
            else:
                t = es.enter_context(tr.nc.sbuf_tensor(nm, shape, dtype))
            self.t.append(t)
            self.s.append(tr.slot(nm))
        self.i = 0
        self.n = n

    def next(self):
        i = self.i
        self.i = (i + 1) % self.n
        return self.t[i], self.s[i]


def load_panel(tr, wbufs, w_ap, col0, ncols=512):
    wt, ws = wbufs.next()
    src = w_ap[:, col0:col0 + ncols].rearrange("(c p) n -> p c n", p=128)
    tr.dma("pool", wt[:, :, 0:ncols], src, anchor=ws, writes=[ws])
    return wt, ws


def rmsnorm_phase(tr, es0, xT, ng_ap, HT, ntok, tile=256):
    nc = tr.nc
    with ExitStack() as es:
        xb = Bufs(tr, es, "xb", [128, KC, tile], F32, 2)
        sqb = Bufs(tr, es, "sqb", [128, KC, tile], BF16, 2)
        rtb = Bufs(tr, es, "rtb", [128, tile], F32, 2)
        rib = Bufs(tr, es, "rib", [128, tile], F32, 2)
        ssp = Bufs(tr, es, "ssp", [128, tile], F32, 2, psum=True)
        ones = es.enter_context(nc.sbuf_tensor("ones0", [128, 128], BF16))
        ngt = es.enter_context(nc.sbuf_tensor("ngt", [128, KC], F32))
        ngs = tr.slot("ngt")
        oness = tr.slot("ones0")
        tr.op("dve", lambda E: E.memset(ones[:], 1.0), writes=[oness])
        tr.dma("sp", ngt[:], ng_ap, anchor=ngs, writes=[ngs])
        hts = tr.slot("HTall")
        for i in range(ntok // tile):
            xt, xs = xb.next()
            tr.dma("sp", xt[:], xT[:, i * tile:(i + 1) * tile].rearrange("(c p) n -> p c n", p=128),
                   anchor=xs, writes=[xs])
            sq, sqs = sqb.next()
            tr.op("act", lambda E, sq=sq, xt=xt: E.activation(out=sq[:], in_=xt[:], func=AF.Square),
                  reads=[xs], writes=[sqs])
            sp_, sps = ssp.next()
            fs = [(lambda E, c=c, sp_=sp_, sq=sq: E.matmul(sp_[:], ones[:], sq[:, c, :], start=(c == 0),
                                                           stop=(c == KC - 1))) for c in range(KC)]
            tr.mm(fs, reads=[sqs, oness], writes=[sps])
            rt, rts = rtb.next()
            tr.op("act", lambda E, rt=rt, sp_=sp_: E.activation(out=rt[:], in_=sp_[:], func=AF.Sqrt,
                                                                bias=EPS, scale=1.0 / D),
                  reads=[sps], writes=[rts])
            ri, ris = rib.next()
            tr.op("dve", lambda E, ri=ri, rt=rt: E.reciprocal(ri[:], rt[:]), reads=[rts], writes=[ris])
            for c in range(KC):
                tr.op("dve", lambda E, c=c, xt=xt, ri=ri, i=i: E.scalar_tensor_tensor(
                    out=HT[:, c, i * tile:(i + 1) * tile], in0=xt[:, c, :], scalar=ngt[:, c:c + 1], in1=ri[:],
                    op0=ALU.mult, op1=ALU.mult), reads=[xs, ris, ngs])
        tr.run()


def perm_views(g, ps_ap, n):
    d = GROUPS[g][1]
    if d == 1:
        return ps_ap
    return ps_ap.rearrange("p (i r) -> p r i", r=d)


L0_STOP = 99
LAST = {}


def build_l0(nc, tr, es0, io):
    xT, w_in, gq_ap, gk_ap, ng_ap, tb_ap, w_out, x1T = (io[k] for k in
                                                         ("xT", "w_in", "gq", "gk", "ng", "tb", "w_out", "x1T"))
    QS = [nc.dram_tensor("QS%d" % g, [16, 128, RG[g] * NB[g] * 128], BF16).ap() for g in range(3)]
    KS = [nc.dram_tensor("KS%d" % g, [16, 128, RG[g] * (NB[g] + 1) * 128], BF16).ap() for g in range(3)]
    VS = [nc.dram_tensor("VS%d" % g, [RG[g] * (NB[g] + 1), 128, D], BF16).ap() for g in range(3)]
    ZS = nc.dram_tensor("ZS", [16, 128, NTOK], F32).ap()
    qs_s = [[tr.slot("qs") for h in range(16)] for g in range(3)]
    ks_s = [[tr.slot("ks") for h in range(16)] for g in range(3)]
    vs_s = [tr.slot("vs") for g in range(3)]
    zs_s = [tr.slot("zs") for h in range(16)]

    with ExitStack() as esA:
        HT = esA.enter_context(nc.sbuf_tensor("HT", [128, KC, NH + NTOK], BF16))
        rmsnorm_phase(tr, es0, xT, ng_ap, HT, NH + NTOK)
        if L0_STOP <= 1:
            return

        with ExitStack() as es:
            wb = Bufs(tr, es, "wb", [128, KC, 512], BF16, 2)
            mp = Bufs(tr, es, "mp", [128, 512], F32, 3, psum=True)
            sp2 = Bufs(tr, es, "sp2", [128, 512], F32, 2, psum=True)
            sqb = Bufs(tr, es, "sq", [128, 512], BF16, 2)
            rtb = Bufs(tr, es, "rt", [128, 512], F32, 2)
            rib = Bufs(tr, es, "ri", [128, 512], F32, 2)
            qst = Bufs(tr, es, "qst", [128, 2048], BF16, 2)
            kst = Bufs(tr, es, "kst", [128, 4096], BF16, 2)
            vtb = Bufs(tr, es, "vt", [128, 512], BF16, 4)
            zst = Bufs(tr, es, "zst", [128, NTOK], F32, 1)
            ones = es.enter_context(nc.sbuf_tensor("ones1", [128, 128], BF16))
            gt = es.enter_context(nc.sbuf_tensor("gt", [128, 6], F32))
            oness = tr.slot("ones1")
            gts = tr.slot("gt")
            tr.op("dve", lambda E: E.memset(ones[:], 1.0), writes=[oness])
            tr.dma("sp", gt[:, 0:3], gq_ap, anchor=gts, writes=[gts])
            tr.dma("sp", gt[:, 3:6], gk_ap, anchor=gts, writes=[gts])
            tr.op("dve", lambda E: E.tensor_scalar(gt[:, 0:3], gt[:, 0:3], 128.0 ** -0.5, None, ALU.mult),
                  reads=[gts], writes=[gts])

            def gemm_fm(wt, ws, cb, tok0, n):
                pt, pss = mp.next()
                fs = [(lambda E, c=c, pt=pt: E.matmul(pt[:, 0:n], wt[:, c, cb * 128:(cb + 1) * 128],
                                                      HT[:, c, tok0:tok0 + n], start=(c == 0), stop=(c == KC - 1)))
                      for c in range(KC)]
                tr.mm(fs, reads=[ws], writes=[pss])
                return pt, pss

            def qk_epilogue(pt, pss, n, g, gidx, out_view, st_s):
                sq, sqs = sqb.next()
                tr.op("act", lambda E: E.activation(out=sq[:, 0:n], in_=pt[:, 0:n], func=AF.Square),
                      reads=[pss], writes=[sqs])
                st_, sts = sp2.next()
                tr.mm([lambda E: E.matmul(st_[:, 0:n], ones[:], sq[:, 0:n], start=True, stop=True)],
                      reads=[sqs, oness], writes=[sts])
                rt, rts = rtb.next()
                tr.op("act", lambda E: E.activation(out=rt[:, 0:n], in_=st_[:, 0:n], func=AF.Sqrt, bias=EPS,
                                                    scale=1.0 / 128), reads=[sts], writes=[rts])
                ri, ris = rib.next()
                tr.op("dve", lambda E: E.reciprocal(ri[:, 0:n], rt[:, 0:n]), reads=[rts], writes=[ris])
                tr.op("dve", lambda E: E.scalar_tensor_tensor(
                    out=out_view, in0=perm_views(g, pt[:, 0:n], n), scalar=gt[:, gidx:gidx + 1],
                    in1=perm_views(g, ri[:, 0:n], n), op0=ALU.mult, op1=ALU.mult),
                    reads=[pss, ris, gts], writes=[st_s])

            def st_view(st, g, nblk, blk0, tt_local):
                R = RG[g]
                v = st[:, 0:R * nblk * 128].rearrange("p (r b j) -> p r b j", r=R, b=nblk)
                if g == 0:
                    return v[:, 0, blk0 + 4 * tt_local:blk0 + 4 * tt_local + 4, :].rearrange("p b j -> p (b j)")
                if g == 1:
                    return v[:, :, blk0 + tt_local, :]
                return v[:, :, blk0, 32 * tt_local:32 * tt_local + 32]

            npanel = 40
            nxt = load_panel(tr, wb, w_in, 0)
            for p in range(npanel):
                wt, ws = nxt
                if p + 1 < npanel:
                    nxt = load_panel(tr, wb, w_in, (p + 1) * 512)
                if p < 36:
                    g, s, hq = p // 12, (p % 12) // 4, p % 4
                else:
                    g, s, hq = None, 3, p - 36
                if s == 0:
                    for cb in range(4):
                        h = hq * 4 + cb
                        st, sts_ = qst.next()
                        for tl in range(4):
                            pt, pss = gemm_fm(wt, ws, cb, NH + 512 * tl, 512)
                            qk_epilogue(pt, pss, 512, g, g, st_view(st, g, NB[g], 0, tl), sts_)
                        n = RG[g] * NB[g] * 128
                        tr.dma("sp", QS[g][h], st[:, 0:n], anchor=sts_, reads=[sts_], writes=[qs_s[g][h]])
                elif s == 1:
                    for cb in range(4):
                        h = hq * 4 + cb
                        st, sts_ = kst.next()
                        nblk = NB[g] + 1
                        if g == 0:
                            pt, pss = gemm_fm(wt, ws, cb, NH - 128, 128)
                            v = st[:, 0:nblk * 128].rearrange("p (b j) -> p b j", b=nblk)
                            qk_epilogue(pt, pss, 128, g, 3 + g, v[:, 0, :], sts_)
                        elif g == 1:
                            pt, pss = gemm_fm(wt, ws, cb, NH - 512, 512)
                            qk_epilogue(pt, pss, 512, g, 3 + g, st_view(st, g, nblk, 0, 0), sts_)
                        else:
                            for tl in range(4):
                                pt, pss = gemm_fm(wt, ws, cb, 512 * tl, 512)
                                qk_epilogue(pt, pss, 512, g, 3 + g, st_view(st, g, nblk, 0, tl), sts_)
                        for tl in range(4):
                            pt, pss = gemm_fm(wt, ws, cb, NH + 512 * tl, 512)
                            qk_epilogue(pt, pss, 512, g, 3 + g, st_view(st, g, nblk, 1, tl), sts_)
                        n = RG[g] * nblk * 128
                        tr.dma("sp", KS[g][h], st[:, 0:n], anchor=sts_, reads=[sts_], writes=[ks_s[g][h]])
                elif s == 2:
                    d = GROUPS[g][1]
                    nblk = NB[g] + 1
                    cnt = 0
                    for r in range(RG[g]):
                        for b in range(nblk):
                            start = NH + d * 128 * (b - 1) + r
                            pt, pss = mp.next()
                            fs = [(lambda E, c=c, pt=pt, start=start, d=d, wt=wt: E.matmul(
                                pt[:], HT[:, c, start:start + 127 * d + 1:d], wt[:, c, :], start=(c == 0),
                                stop=(c == KC - 1))) for c in range(KC)]
                            tr.mm(fs, reads=[ws], writes=[pss])
                            vt, vts = vtb.next()
                            if cnt % 2 == 0:
                                tr.op("act", lambda E, vt=vt, pt=pt: E.activation(out=vt[:], in_=pt[:], func=AF.Copy),
                                      reads=[pss], writes=[vts])
                            else:
                                tr.op("dve", lambda E, vt=vt, pt=pt: E.tensor_copy(vt[:], pt[:]),
                                      reads=[pss], writes=[vts])
                            cnt += 1
                            tr.dma("sp", VS[g][r * nblk + b][:, hq * 512:(hq + 1) * 512], vt[:], anchor=vts,
                                   reads=[vts], writes=[vs_s[g]])
                else:
                    for cb in range(4):
                        h = hq * 4 + cb
                        st, sts_ = zst.next()
                        for tl in range(4):
                            pt, pss = gemm_fm(wt, ws, cb, NH + 512 * tl, 512)
                            tr.op("act", lambda E, st=st, pt=pt, tl=tl: E.activation(
                                out=st[:, 512 * tl:512 * tl + 512], in_=pt[:], func=AF.Silu),
                                reads=[pss], writes=[sts_])
                        tr.dma("sp", ZS[h], st[:], anchor=sts_, reads=[sts_], writes=[zs_s[h]])
            tr.run()

    if L0_STOP <= 2:
        return
    with ExitStack() as esB:
        YT = esB.enter_context(nc.sbuf_tensor("YT", [128, 16, NTOK], BF16))
        yts = tr.slot("YT")
        with ExitStack() as es:
            qh = [Bufs(tr, es, "qh%d" % g, [128, RG[g] * NB[g] * 128], BF16, 2) for g in range(3)]
            kh = [Bufs(tr, es, "kh%d" % g, [128, RG[g] * (NB[g] + 1) * 128], BF16, 2) for g in range(3)]
            vh = [Bufs(tr, es, "vh%d" % g, [128, RG[g] * (NB[g] + 1), 128], BF16, 2) for g in range(3)]
            zh = Bufs(tr, es, "zh", [128, NTOK], F32, 1)
            th = Bufs(tr, es, "th", [128, 3, 4 * 128], F32, 2)
            acc = Bufs(tr, es, "acc", [128, 2, NTOK], F32, 1)
            sps = Bufs(tr, es, "sps", [128, 256], F32, 3, psum=True)
            ops = Bufs(tr, es, "ops", [128, 256], F32, 3, psum=True)
            ssb = Bufs(tr, es, "ssb", [128, 256], F32, 3)
            pb = Bufs(tr, es, "pb", [128, 256], BF16, 3)
            ones = es.enter_context(nc.sbuf_tensor("ones2", [128, 128], BF16))
            oness = tr.slot("ones2")
            tr.op("dve", lambda E: E.memset(ones[:], 1.0), writes=[oness])

            def load_head(h):
                L = {}
                for g in range(3):
                    t, s = qh[g].next()
                    tr.dma("sp", t[:], QS[g][h], anchor=s, reads=[qs_s[g][h]], writes=[s])
                    L["q%d" % g] = (t, s)
                    t, s = kh[g].next()
                    tr.dma("sp", t[:], KS[g][h], anchor=s, reads=[ks_s[g][h]], writes=[s])
                    L["k%d" % g] = (t, s)
                    t, s = vh[g].next()
                    nbt = RG[g] * (NB[g] + 1)
                    for b0 in range(0, nbt, 8):
                        b1 = min(nbt, b0 + 8)
                        tr.dma("sp", t[:, b0:b1, :],
                               VS[g][b0:b1, :, h * 128:(h + 1) * 128].rearrange("b t c -> t b c"), anchor=s,
                               reads=[vs_s[g]], writes=[s])
                    L["v%d" % g] = (t, s)
                t, s = th.next()
                tr.dma("sp", t[:], tb_ap[:, h].rearrange("g k f -> k g f"), anchor=s, writes=[s])
                L["t"] = (t, s)
                return L

            nxt = load_head(0)
            for h in range(16):
                L = nxt
                if h + 1 < 16:
                    nxt = load_head(h + 1)
                at, ats = acc.next()
                tt_, tts = L["t"]
                blocks = []
                for g in range(3):
                    for r in range(RG[g]):
                        for n in range(NB[g]):
                            blocks.append((g, r, n))

                def stage1(blk):
                    g, r, n = blk
                    R, nb = RG[g], NB[g]
                    qt, qs_ = L["q%d" % g]
                    kt, ks_ = L["k%d" % g]
                    qv = qt[:].rearrange("p (r b j) -> p r b j", r=R, b=nb)
                    kv = kt[:].rearrange("p (r b j) -> p r b j", r=R, b=nb + 1)
                    tloc = tt_
                    st_, sts = sps.next()
                    tr.mm([lambda E: E.matmul(st_[:, 0:128], kv[:, r, n, :], qv[:, r, n, :], start=True, stop=True),
                           lambda E: E.matmul(st_[:, 128:256], kv[:, r, n + 1, :], qv[:, r, n, :], start=True,
                                              stop=True)], reads=[qs_, ks_], writes=[sts])
                    sb, sbs = ssb.next()
                    toff = 256 if n == 0 else 0
                    tr.op("dve", lambda E: E.tensor_tensor(sb[:], st_[:], tloc[:, g, toff:toff + 256], ALU.add),
                          reads=[sts, tts], writes=[sbs])
                    pt, pts = pb.next()
                    tr.op("act", lambda E: E.activation(out=pt[:], in_=sb[:], func=AF.Exp),
                          reads=[sbs], writes=[pts])
                    return pt, pts

                def stage2(blk, pt, pts):
                    g, r, n = blk
                    d = GROUPS[g][1]
                    nb = NB[g]
                    vt, vs_ = L["v%d" % g]
                    ot, ots = ops.next()
                    b0 = r * (nb + 1) + n
                    tr.mm([lambda E: E.matmul(ot[:, 0:128], vt[:, b0, :], pt[:, 0:128], start=True, stop=False),
                           lambda E: E.matmul(ot[:, 0:128], vt[:, b0 + 1, :], pt[:, 128:256], start=False, stop=True),
                           lambda E: E.matmul(ot[:, 128:256], ones[:], pt[:, 0:128], start=True, stop=False),
                           lambda E: E.matmul(ot[:, 128:256], ones[:], pt[:, 128:256], start=False, stop=True)],
                          reads=[pts, vs_, oness], writes=[ots])
                    t0 = r + d * 128 * n
                    aloc = at
                    av = aloc[:, :, t0:t0 + 127 * d + 1:d]
                    ov = ot[:].rearrange("p (a j) -> p a j", a=2)
                    if g == 0:
                        tr.op("dve", lambda E: E.tensor_copy(av, ov), reads=[ots], writes=[ats])
                    else:
                        tr.op("dve", lambda E: E.tensor_tensor(av, ov, av, ALU.add), reads=[ots, ats], writes=[ats])

                pend = stage1(blocks[0])
                for bi in range(len(blocks)):
                    cur = pend
                    if bi + 1 < len(blocks):
                        pend = stage1(blocks[bi + 1])
                    stage2(blocks[bi], *cur)
                zt, zs_ = zh.next()
                tr.dma("sp", zt[:], ZS[h], anchor=zs_, reads=[zs_s[h]], writes=[zs_])
                tr.op("dve", lambda E, at=at: E.reciprocal(at[:, 1, :], at[:, 1, :]), reads=[ats], writes=[ats])
                tr.op("dve", lambda E, at=at: E.tensor_tensor(at[:, 0, :], at[:, 0, :], at[:, 1, :], ALU.mult),
                      reads=[ats], writes=[ats])
                tr.op("dve", lambda E, at=at, zt=zt, h=h: E.tensor_tensor(YT[:, h, :], at[:, 0, :], zt[:], ALU.mult),
                      reads=[ats, zs_], writes=[yts])
            if "dbg" in io:
                tr.dma("sp", io["dbg"], YT[:], anchor=yts, reads=[yts])
            tr.run()

        if L0_STOP <= 3:
            return
        outproj_phase(tr, YT, w_out, xT, NH, x1T, None)


def outproj_phase(tr, YT, w_out, resT, res_off, outT, bias_ap):
    nc = tr.nc
    with ExitStack() as es:
        wb = Bufs(tr, es, "wo", [128, KC, 512], BF16, 2)
        mp = Bufs(tr, es, "mo", [128, 512], F32, 3, psum=True)
        xb = Bufs(tr, es, "xr", [128, 512], F32, 3)
        ob = Bufs(tr, es, "ob", [128, 512], F32, 3)
        bt = None
        if bias_ap is not None:
            bt = es.enter_context(nc.sbuf_tensor("bo", [128, KC], F32))
            bs = tr.slot("bo")
            tr.dma("sp", bt[:], bias_ap, anchor=bs, writes=[bs])
        nxt = load_panel(tr, wb, w_out, 0)
        for p in range(4):
            wt, ws = nxt
            if p + 1 < 4:
                nxt = load_panel(tr, wb, w_out, (p + 1) * 512)
            for cb in range(4):
                m = p * 4 + cb
                for tl in range(NTOK // 512):
                    xt, xs = xb.next()
                    tr.dma("sp", xt[:], resT[m * 128:(m + 1) * 128, res_off + 512 * tl:res_off + 512 * tl + 512],
                           anchor=xs, writes=[xs])
                    pt, pss = mp.next()
                    fs = [(lambda E, c=c, pt=pt, wt=wt, cb=cb, tl=tl: E.matmul(
                        pt[:], wt[:, c, cb * 128:(cb + 1) * 128], YT[:, c, 512 * tl:512 * tl + 512],
                        start=(c == 0), stop=(c == KC - 1))) for c in range(KC)]
                    tr.mm(fs, reads=[ws], writes=[pss])
                    ot, os_ = ob.next()
                    if bt is None:
                        tr.op("dve", lambda E, ot=ot, pt=pt, xt=xt: E.tensor_tensor(ot[:], pt[:], xt[:], ALU.add),
                              reads=[pss, xs], writes=[os_])
                    else:
                        tr.op("dve", lambda E, ot=ot, pt=pt, xt=xt, m=m: E.scalar_tensor_tensor(
                            out=ot[:], in0=pt[:], scalar=bt[:, m:m + 1], in1=xt[:], op0=ALU.add, op1=ALU.add),
                            reads=[pss, xs, bs], writes=[os_])
                    tr.dma("sp", outT[m * 128:(m + 1) * 128, 512 * tl:512 * tl + 512], ot[:], anchor=os_,
                           reads=[os_])
        tr.run()


def _t5_bucket(dist):
    dist = np.asarray(dist, dtype=np.int64)
    max_exact = 16
    ratio = np.log(np.maximum(dist, 1).astype(np.float32) / np.float32(max_exact)) / np.float32(math.log(2048 / 16))
    large = max_exact + (ratio.astype(np.float32) * np.float32(16)).astype(np.int32)
    large = np.minimum(large, 31)
    return np.where(dist < max_exact, dist, large)


def bias_tables(rel_bias, first_masked):
    k = np.arange(128)[:, None]
    q = np.arange(128)[None, :]
    out = np.full((3, 16, 128, 4, 128), NEG, dtype=np.float32)
    for g, (win, d) in enumerate(GROUPS):
        step_prev = q + 128 - k
        step_own = q - k
        ok_prev = step_prev <= 128
        ok_own = step_own >= 0
        bp = _t5_bucket(np.clip(step_prev, 0, 128) * d)
        bo = _t5_bucket(np.clip(step_own, 0, 128) * d)
        for h in range(16):
            col = rel_bias[:, g * 16 + h]
            tp = np.where(ok_prev, col[bp], np.float32(NEG)).astype(np.float32)
            to = np.where(ok_own, col[bo], np.float32(NEG)).astype(np.float32)
            out[g, h, :, 0, :] = tp
            out[g, h, :, 1, :] = to
            out[g, h, :, 2, :] = np.float32(NEG) if first_masked else tp
            out[g, h, :, 3, :] = to
    return out.reshape(3, 16, 128, 512)


def make_l0():
    nc = bass.Bass("TRN2", target_bir_lowering=False)
    io = {
        "xT": nc.dram_tensor("xT", [D, NH + NTOK], F32, kind="ExternalInput").ap(),
        "w_in": nc.dram_tensor("w_in", [D, 20480], F32, kind="ExternalInput").ap(),
        "gq": nc.dram_tensor("gq", [128, 3], F32, kind="ExternalInput").ap(),
        "gk": nc.dram_tensor("gk", [128, 3], F32, kind="ExternalInput").ap(),
        "ng": nc.dram_tensor("ng", [128, KC], F32, kind="ExternalInput").ap(),
        "tb": nc.dram_tensor("tb", [3, 16, 128, 512], F32, kind="ExternalInput").ap(),
        "w_out": nc.dram_tensor("w_out", [D, D], F32, kind="ExternalInput").ap(),
        "x1T": nc.dram_tensor("x1T", [D, NTOK], F32, kind="ExternalOutput").ap(),
    }
    with ExitStack() as es0:
        tr = TR(nc, es0)
        build_l0(nc, tr, es0, io)
    LAST["tr"] = tr
    return nc


def l0_inputs(x, norm_g, rel_bias, a_w_in, a_q_gain, a_k_gain, a_w_out):
    maps = []
    w_in = np.ascontiguousarray(a_w_in[0])
    w_out = np.ascontiguousarray(a_w_out[0])
    gq = np.ascontiguousarray(a_q_gain[0].T)
    gk = np.ascontiguousarray(a_k_gain[0].T)
    ng = np.ascontiguousarray(norm_g[0].reshape(KC, 128).T)
    tbs = [bias_tables(rel_bias, True), bias_tables(rel_bias, False)]
    for c in range(8):
        b, ch = c // 4, c % 4
        t0 = ch * NTOK
        xT = np.zeros((D, NH + NTOK), np.float32)
        xT[:, NH:] = x[b, t0:t0 + NTOK].T
        if ch > 0:
            xT[:, :NH] = x[b, t0 - NH:t0].T
        maps.append({"xT": xT, "w_in": w_in, "gq": gq, "gk": gk, "ng": ng,
                     "tb": tbs[0] if ch == 0 else tbs[1], "w_out": w_out})
    return maps


def run_l0(x, norm_g, rel_bias, a_w_in, a_q_gain, a_k_gain, a_w_out):
    nc = make_l0()
    maps = l0_inputs(x, norm_g, rel_bias, a_w_in, a_q_gain, a_k_gain, a_w_out)
    res = run_bass_kernel_spmd(nc, maps, core_ids=list(range(8)))
    x1 = np.empty_like(x)
    for c in range(8):
        b, ch = c // 4, c % 4
        x1[b, ch * NTOK:(ch + 1) * NTOK] = res.results[c]["x1T"].T
    return x1


HC = 32


def build_l1(nc, tr, es0, io):
    x1T, ng_ap, w_in, bin_ap, cw_ap, cb_ap, lg_ap, lb_ap, w_out, bo_ap, hm_ap, outT = (io[k] for k in (
        "x1T", "ng", "w_in", "b_in", "cw", "cb", "lg", "lb", "w_out", "b_out", "hm", "outT"))
    NT = HC + NTOK
    VC = nc.dram_tensor("VC", [16, 128, NTOK], F32).ap()
    ZC = nc.dram_tensor("ZC", [16, 128, NTOK], F32).ap()
    vc_s = [tr.slot("vc") for c in range(16)]
    zc_s = [tr.slot("zc") for c in range(16)]
    MS = nc.dram_tensor("MS", [2, 128, NTOK], F32).ap()
    ms_s = tr.slot("ms")

    with ExitStack() as esA:
        HT = esA.enter_context(nc.sbuf_tensor("HT1", [128, KC, NT], BF16))
        rmsnorm_phase(tr, es0, x1T, ng_ap, HT, NT, tile=NT // 8)
        with ExitStack() as es:
            wb = Bufs(tr, es, "w3", [128, KC, 384], BF16, 2)
            mp = Bufs(tr, es, "m1", [128, 512], F32, 4, psum=True)
            sp_ = Bufs(tr, es, "s1", [128, 512], F32, 2, psum=True)
            ub = Bufs(tr, es, "ub", [128, NT], F32, 2)
            vb = Bufs(tr, es, "vb", [128, NTOK], F32, 2)
            zb = Bufs(tr, es, "zb", [128, NTOK], F32, 1)
            sgb = Bufs(tr, es, "sg", [128, 512], F32, 2)
            sqv = Bufs(tr, es, "sqv", [128, NTOK], F32, 1)
            vsum = es.enter_context(nc.sbuf_tensor("vsum", [128, 2, NTOK], F32))
            vss = tr.slot("vsum")
            mst = es.enter_context(nc.sbuf_tensor("mst", [128, 2, NTOK], F32))
            mss = tr.slot("mst")
            onesf = es.enter_context(nc.sbuf_tensor("onesf", [128, 128], BF16))
            oness = tr.slot("onesf")
            vhi = es.enter_context(nc.sbuf_tensor("vhi", [128, 2, NTOK], BF16))
            vlo = es.enter_context(nc.sbuf_tensor("vlo", [128, 2, NTOK], BF16))
            vhs = tr.slot("vhi")
            cst = es.enter_context(nc.sbuf_tensor("cst", [128, 48 + 16 * CONVW + 16 + 1], F32))
            cs = tr.slot("cst")
            BI, CW, CB, HM = 0, 48, 48 + 16 * CONVW, 48 + 16 * CONVW + 16
            tr.op("dve", lambda E: E.memset(onesf[:], 1.0), writes=[oness])
            tr.dma("sp", cst[:, BI:BI + 48], bin_ap, anchor=cs, writes=[cs])
            tr.dma("sp", cst[:, CW:CW + 16 * CONVW], cw_ap, anchor=cs, writes=[cs])
            tr.dma("sp", cst[:, CB:CB + 16], cb_ap, anchor=cs, writes=[cs])
            tr.dma("sp", cst[:, HM:HM + 1], hm_ap, anchor=cs, writes=[cs])

            def load3(c):
                wt, ws = wb.next()
                for k in range(3):
                    src = w_in[:, k * D + c * 128:k * D + (c + 1) * 128].rearrange("(c p) n -> p c n", p=128)
                    tr.dma("pool", wt[:, :, k * 128:(k + 1) * 128], src, anchor=ws, writes=[ws])
                return wt, ws

            def gemm(wt, ws, k, tok0, n):
                pt, pss = mp.next()
                fs = [(lambda E, cc=cc, pt=pt: E.matmul(pt[:, 0:n], wt[:, cc, k * 128:(k + 1) * 128],
                                                        HT[:, cc, tok0:tok0 + n], start=(cc == 0),
                                                        stop=(cc == KC - 1))) for cc in range(KC)]
                tr.mm(fs, reads=[ws], writes=[pss])
                return pt, pss

            tiles = [(0, 512), (512, 512), (1024, 512), (1536, 512), (2048, NT - 2048)]
            nxt = load3(0)
            for c in range(16):
                wt, ws = nxt
                if c + 1 < 16:
                    nxt = load3(c + 1)
                ut, us = ub.next()
                for (t0, n) in tiles:
                    pa, pas = gemm(wt, ws, 0, t0, n)
                    pg, pgs = gemm(wt, ws, 1, t0, n)
                    sg, sgs = sgb.next()
                    tr.op("act", lambda E, sg=sg, pg=pg, n=n, c=c: E.activation(
                        out=sg[:, 0:n], in_=pg[:, 0:n], func=AF.Sigmoid, bias=cst[:, BI + 16 + c:BI + 17 + c]),
                        reads=[pgs, cs], writes=[sgs])
                    tr.op("dve", lambda E, ut=ut, pa=pa, sg=sg, t0=t0, n=n, c=c: E.scalar_tensor_tensor(
                        out=ut[:, t0:t0 + n], in0=pa[:, 0:n], scalar=cst[:, BI + c:BI + c + 1], in1=sg[:, 0:n],
                        op0=ALU.add, op1=ALU.mult), reads=[pas, sgs, cs], writes=[us])
                tr.op("dve", lambda E, ut=ut: E.tensor_scalar(ut[:, 0:HC], ut[:, 0:HC], cst[:, HM:HM + 1], None,
                                                              ALU.mult), reads=[us, cs], writes=[us])
                zt, zs_ = zb.next()
                for tl in range(4):
                    pz, pzs = gemm(wt, ws, 2, HC + 512 * tl, 512)
                    tr.op("act", lambda E, zt=zt, pz=pz, tl=tl, c=c: E.activation(
                        out=zt[:, 512 * tl:512 * tl + 512], in_=pz[:], func=AF.Silu,
                        bias=cst[:, BI + 32 + c:BI + 33 + c]), reads=[pzs, cs], writes=[zs_])
                tr.dma("sp", ZC[c], zt[:], anchor=zs_, reads=[zs_], writes=[zc_s[c]])
                vt, vs_ = vb.next()
                off = HC - (CONVW - 1)
                tr.op("dve", lambda E, vt=vt, ut=ut, c=c: E.tensor_scalar(
                    vt[:], ut[:, off:off + NTOK], cst[:, CW + c * CONVW:CW + c * CONVW + 1],
                    cst[:, CB + c:CB + c + 1], ALU.mult, ALU.add), reads=[us, cs], writes=[vs_])
                for j in range(1, CONVW):
                    tr.op("dve", lambda E, vt=vt, ut=ut, c=c, j=j: E.scalar_tensor_tensor(
                        out=vt[:], in0=ut[:, off + j:off + j + NTOK],
                        scalar=cst[:, CW + c * CONVW + j:CW + c * CONVW + j + 1], in1=vt[:],
                        op0=ALU.mult, op1=ALU.add), reads=[us, cs, vs_], writes=[vs_])
                tr.dma("sp", VC[c], vt[:], anchor=vs_, reads=[vs_], writes=[vc_s[c]])
                sq, sqs = sqv.next()
                tr.op("act", lambda E, sq=sq, vt=vt: E.activation(out=sq[:], in_=vt[:], func=AF.Square),
                      reads=[vs_], writes=[sqs])
                if c == 0:
                    tr.op("dve", lambda E, vt=vt: E.tensor_copy(vsum[:, 0, :], vt[:]), reads=[vs_], writes=[vss])
                    tr.op("dve", lambda E, sq=sq: E.tensor_copy(vsum[:, 1, :], sq[:]), reads=[sqs], writes=[vss])
                else:
                    tr.op("dve", lambda E, vt=vt: E.tensor_tensor(vsum[:, 0, :], vsum[:, 0, :], vt[:], ALU.add),
                          reads=[vs_, vss], writes=[vss])
                    tr.op("dve", lambda E, sq=sq: E.tensor_tensor(vsum[:, 1, :], vsum[:, 1, :], sq[:], ALU.add),
                          reads=[sqs, vss], writes=[vss])
            tr.op("act", lambda E: E.activation(out=vhi[:], in_=vsum[:], func=AF.Copy), reads=[vss], writes=[vhs])
            tr.op("dve", lambda E: E.tensor_tensor(vlo[:], vsum[:], vhi[:], ALU.subtract), reads=[vss, vhs],
                  writes=[vhs])
            for tl in range(4):
                sl = slice(512 * tl, 512 * tl + 512)
                p1, p1s = sp_.next()
                tr.mm([lambda E, p1=p1, sl=sl: E.matmul(p1[:], onesf[:], vhi[:, 0, sl], start=True, stop=False),
                       lambda E, p1=p1, sl=sl: E.matmul(p1[:], onesf[:], vlo[:, 0, sl], start=False, stop=True)],
                      reads=[vhs, oness], writes=[p1s])
                p2, p2s = sp_.next()
                tr.mm([lambda E, p2=p2, sl=sl: E.matmul(p2[:], onesf[:], vhi[:, 1, sl], start=True, stop=False),
                       lambda E, p2=p2, sl=sl: E.matmul(p2[:], onesf[:], vlo[:, 1, sl], start=False, stop=True)],
                      reads=[vhs, oness], writes=[p2s])
                tr.op("dve", lambda E, p1=p1, sl=sl: E.tensor_scalar(mst[:, 0, sl], p1[:], 1.0 / D, None, ALU.mult),
                      reads=[p1s], writes=[mss])
                tr.op("dve", lambda E, sl=sl: E.tensor_tensor(mst[:, 1, sl], mst[:, 0, sl], mst[:, 0, sl], ALU.mult),
                      reads=[mss], writes=[mss])
                tr.op("dve", lambda E, p2=p2, sl=sl: E.scalar_tensor_tensor(
                    out=mst[:, 1, sl], in0=p2[:], scalar=1.0 / D, in1=mst[:, 1, sl], op0=ALU.mult,
                    op1=ALU.subtract), reads=[p2s, mss], writes=[mss])
                tr.op("act", lambda E, sl=sl: E.activation(out=mst[:, 1, sl], in_=mst[:, 1, sl], func=AF.Sqrt,
                                                           bias=EPS, scale=1.0), reads=[mss], writes=[mss])
                tr.op("dve", lambda E, sl=sl: E.reciprocal(mst[:, 1, sl], mst[:, 1, sl]), reads=[mss], writes=[mss])
            tr.dma("sp", MS.rearrange("a p n -> p a n"), mst[:], anchor=mss, reads=[mss], writes=[ms_s])
            tr.run()

    with ExitStack() as esB:
        YT = esB.enter_context(nc.sbuf_tensor("YT1", [128, 16, NTOK], BF16))
        yts = tr.slot("YT1")
        with ExitStack() as es:
            vin = Bufs(tr, es, "vin", [128, NTOK], F32, 2)
            zin = Bufs(tr, es, "zin", [128, NTOK], F32, 2)
            tb_ = Bufs(tr, es, "tb", [128, NTOK], F32, 2)
            mst = es.enter_context(nc.sbuf_tensor("mst2", [128, 2, NTOK], F32))
            mss = tr.slot("mst2")
            cst = es.enter_context(nc.sbuf_tensor("cst2", [128, 32], F32))
            cs = tr.slot("cst2")
            tr.dma("sp", cst[:, 0:16], lg_ap, anchor=cs, writes=[cs])
            tr.dma("sp", cst[:, 16:32], lb_ap, anchor=cs, writes=[cs])
            tr.dma("sp", mst[:], MS.rearrange("a p n -> p a n"), anchor=mss, reads=[ms_s], writes=[mss])
            for c in range(16):
                vt, vs_ = vin.next()
                tr.dma("sp", vt[:], VC[c], anchor=vs_, reads=[vc_s[c]], writes=[vs_])
                zt, zs_ = zin.next()
                tr.dma("sp", zt[:], ZC[c], anchor=zs_, reads=[zc_s[c]], writes=[zs_])
                tt, ts_ = tb_.next()
                tr.op("dve", lambda E, tt=tt, vt=vt: E.tensor_tensor(tt[:], vt[:], mst[:, 0, :], ALU.subtract),
                      reads=[vs_, mss], writes=[ts_])
                tr.op("dve", lambda E, tt=tt: E.tensor_tensor(tt[:], tt[:], mst[:, 1, :], ALU.mult),
                      reads=[ts_, mss], writes=[ts_])
                tr.op("act", lambda E, tt=tt, c=c: E.activation(out=tt[:], in_=tt[:], func=AF.Silu,
                                                                bias=cst[:, 16 + c:17 + c], scale=cst[:, c:c + 1]),
                      reads=[ts_, cs], writes=[ts_])
                tr.op("dve", lambda E, tt=tt, zt=zt, c=c: E.tensor_tensor(YT[:, c, :], tt[:], zt[:], ALU.mult),
                      reads=[ts_, zs_], writes=[yts])
            tr.run()
        outproj_phase(tr, YT, w_out, x1T, HC, outT, bo_ap)


def make_l1():
    nc = bass.Bass("TRN2", target_bir_lowering=False)
    io = {
        "x1T": nc.dram_tensor("x1T", [D, HC + NTOK], F32, kind="ExternalInput").ap(),
        "ng": nc.dram_tensor("ng", [128, KC], F32, kind="ExternalInput").ap(),
        "w_in": nc.dram_tensor("w_in", [D, 3 * D], F32, kind="ExternalInput").ap(),
        "b_in": nc.dram_tensor("b_in", [128, 48], F32, kind="ExternalInput").ap(),
        "cw": nc.dram_tensor("cw", [128, 16 * CONVW], F32, kind="ExternalInput").ap(),
        "cb": nc.dram_tensor("cb", [128, 16], F32, kind="ExternalInput").ap(),
        "lg": nc.dram_tensor("lg", [128, 16], F32, kind="ExternalInput").ap(),
        "lb": nc.dram_tensor("lb", [128, 16], F32, kind="ExternalInput").ap(),
        "w_out": nc.dram_tensor("w_out", [D, D], F32, kind="ExternalInput").ap(),
        "b_out": nc.dram_tensor("b_out", [128, 16], F32, kind="ExternalInput").ap(),
        "hm": nc.dram_tensor("hm", [128, 1], F32, kind="ExternalInput").ap(),
        "outT": nc.dram_tensor("outT", [D, NTOK], F32, kind="ExternalOutput").ap(),
    }
    with ExitStack() as es0:
        tr = TR(nc, es0)
        build_l1(nc, tr, es0, io)
    LAST["tr"] = tr
    return nc


def _pc(v):
    return np.ascontiguousarray(v.reshape(16, 128).T)


def l1_inputs(x1, norm_g, b_w_in, b_b_in, b_conv_w, b_conv_b, b_ln_g, b_ln_b, b_w_out, b_b_out):
    w_in = np.ascontiguousarray(b_w_in[0])
    w_out = np.ascontiguousarray(b_w_out[0])
    b_in = np.ascontiguousarray(b_b_in[0].reshape(48, 128).T)
    cw = np.ascontiguousarray(b_conv_w[0].T.reshape(16, 128, CONVW).transpose(1, 0, 2).reshape(128, 16 * CONVW))
    common = {"ng": _pc(norm_g[1]), "w_in": w_in, "b_in": b_in, "cw": cw, "cb": _pc(b_conv_b[0]),
              "lg": _pc(b_ln_g[0]), "lb": _pc(b_ln_b[0]), "w_out": w_out, "b_out": _pc(b_b_out[0])}
    maps = []
    for c in range(8):
        b, ch = c // 4, c % 4
        t0 = ch * NTOK
        xT = np.zeros((D, HC + NTOK), np.float32)
        xT[:, HC:] = x1[b, t0:t0 + NTOK].T
        if ch > 0:
            xT[:, :HC] = x1[b, t0 - HC:t0].T
        m = dict(common)
        m["x1T"] = xT
        m["hm"] = np.full((128, 1), 0.0 if ch == 0 else 1.0, np.float32)
        maps.append(m)
    return maps


def run_l1(x1, norm_g, b_w_in, b_b_in, b_conv_w, b_conv_b, b_ln_g, b_ln_b, b_w_out, b_b_out):
    nc = make_l1()
    maps = l1_inputs(x1, norm_g, b_w_in, b_b_in, b_conv_w, b_conv_b, b_ln_g, b_ln_b, b_w_out, b_b_out)
    res = run_bass_kernel_spmd(nc, maps, core_ids=list(range(8)))
    out = np.empty_like(x1)
    for c in range(8):
        b, ch = c // 4, c % 4
        out[b, ch * NTOK:(ch + 1) * NTOK] = res.results[c]["outT"].T
    return out


def kernel(x, norm_g, rel_bias, a_w_in, a_q_gain, a_k_gain, a_w_out,
           b_w_in, b_b_in, b_conv_w, b_conv_b, b_ln_g, b_ln_b, b_w_out, b_b_out):
    args = [np.asarray(a, dtype=np.float32) for a in (
        x, norm_g, rel_bias, a_w_in, a_q_gain, a_k_gain, a_w_out,
        b_w_in, b_b_in, b_conv_w, b_conv_b, b_ln_g, b_ln_b, b_w_out, b_b_out)]
    (x, norm_g, rel_bias, a_w_in, a_q_gain, a_k_gain, a_w_out,
     b_w_in, b_b_in, b_conv_w, b_conv_b, b_ln_g, b_ln_b, b_w_out, b_b_out) = args
    x1 = run_l0(x, norm_g, rel_bias, a_w_in, a_q_gain, a_k_gain, a_w_out)
    return run_l1(x1, norm_g, b_w_in, b_b_in, b_conv_w, b_conv_b, b_ln_g, b_ln_b, b_w_out, b_b_out)
```
